# Optimizing a Trainium2 kernel written in Bass

```python
import jax, jax.numpy as jnp
from jax import lax
import numpy as np

D_MODEL = 4096
BATCH = 1
SEQ = 8192
DEPTH = 2

N_HEADS = 32
N_KV_HEADS = 8
HEAD_DIM = 128
Q_GROUP = N_HEADS // N_KV_HEADS
ATTN_WIDTH = N_HEADS * HEAD_DIM
KV_WIDTH = N_KV_HEADS * HEAD_DIM
AXIS_ROT_DIM = HEAD_DIM // 2
ROPE_THETA = 10000.0
Q_BLOCK = 128
GRID_W = 64

POOL_WINDOWS = (2, 4, 8, 16)
N_POOL_GROUPS = 4
POOL_WIDTH = 2048
POOL_GROUP_IN = POOL_WIDTH // N_POOL_GROUPS
POOL_GROUP_OUT = D_MODEL // N_POOL_GROUPS

IN_WIDTH = ATTN_WIDTH + 2 * KV_WIDTH + POOL_WIDTH + 2 * D_MODEL

D_FF = -(-8 * D_MODEL // (3 * 256)) * 256

D_PLE = 256

EPS = 1e-6

kernel_name = "hybrid_gqa_axialrope_multipool_swiglu_ple"


def rms_norm(x, g):
    xf = x.astype(jnp.float32)
    y = xf * lax.rsqrt(jnp.mean(xf * xf, axis=-1, keepdims=True) + EPS)
    return (y * g.astype(jnp.float32)).astype(x.dtype)


def axial_rope_tables(seq):
    rows = seq // GRID_W
    row = jnp.broadcast_to(jnp.arange(rows)[:, None], (rows, GRID_W)).reshape(seq)
    col = jnp.broadcast_to(jnp.arange(GRID_W)[None, :], (rows, GRID_W)).reshape(seq)
    inv_freq = ROPE_THETA ** (-jnp.arange(0, AXIS_ROT_DIM, 2, dtype=jnp.float32) / AXIS_ROT_DIM)
    pos = jnp.stack([row, col], axis=-1).astype(jnp.float32)
    ang = pos[:, :, None] * inv_freq[None, None, :]
    return jnp.cos(ang), jnp.sin(ang)


def apply_axial_rope(x, cos, sin):
    b, s, h, d = x.shape
    xf = x.astype(jnp.float32).reshape(b, s, h, 2, 2, AXIS_ROT_DIM // 2)
    x1, x2 = xf[..., 0, :], xf[..., 1, :]
    c, sn = cos[None, :, None], sin[None, :, None]
    out = jnp.stack([x1 * c - x2 * sn, x2 * c + x1 * sn], axis=-2)
    return out.reshape(b, s, h, d).astype(x.dtype)


def block_attention(q, k, v):
    b, s = q.shape[:2]
    nb = s // Q_BLOCK
    qb = q.reshape(b, nb, Q_BLOCK, N_KV_HEADS, Q_GROUP, HEAD_DIM).transpose(1, 0, 2, 3, 4, 5)
    scale = HEAD_DIM ** -0.5

    def one_block(qblk):
        sc = jnp.einsum('bqkgd,bskd->bkgqs', qblk, k,
                        preferred_element_type=jnp.float32) * scale
        pr = jax.nn.softmax(sc, axis=-1).astype(v.dtype)
        return jnp.einsum('bkgqs,bskd->bqkgd', pr, v)

    out = lax.map(one_block, qb)
    return out.transpose(1, 0, 2, 3, 4, 5).reshape(b, s, ATTN_WIDTH)


def multiscale_pool(u, w_pool, pool_scale):
    b, s, _ = u.shape
    ug = u.astype(jnp.float32).reshape(b, s, N_POOL_GROUPS, POOL_GROUP_IN)
    csum = jnp.concatenate([jnp.zeros((b, 1, N_POOL_GROUPS, POOL_GROUP_IN), jnp.float32),
                            jnp.cumsum(ug, axis=1)], axis=1)
    t = jnp.arange(s)[:, None]
    half = jnp.array(POOL_WINDOWS, dtype=jnp.int32)[None, :] // 2
    lo = jnp.clip(t - half, 0, s)
    hi = jnp.clip(t + half, 0, s)
    grp = jnp.arange(N_POOL_GROUPS)[None, :]
    window_sum = csum[:, hi, grp, :] - csum[:, lo, grp, :]
    count = (hi - lo).astype(jnp.float32)[None, :, :, None]
    delta = (window_sum / count - ug).astype(u.dtype)
    y = jnp.einsum('bsgc,gcd->bsgd', delta, w_pool).reshape(b, s, D_MODEL)
    return y * pool_scale


def hybrid_layer(x, p_i, w_in, q_norm, k_norm, w_pool, pool_scale, w_out,
                 norm_mix_pre, norm_mix_post, norm_ffn_pre, norm_ffn_post,
                 w_ffn_gate, w_ffn_up, w_ffn_down, w_ple_in, w_ple_gate, norm_ple, cos, sin):
    b, s, _ = x.shape
    h = rms_norm(x, norm_mix_pre)
    z = h @ w_in
    o1 = ATTN_WIDTH
    o2 = o1 + KV_WIDTH
    o3 = o2 + KV_WIDTH
    o4 = o3 + POOL_WIDTH
    o5 = o4 + D_MODEL
    q = z[..., :o1].reshape(b, s, N_HEADS, HEAD_DIM)
    k = z[..., o1:o2].reshape(b, s, N_KV_HEADS, HEAD_DIM)
    v = z[..., o2:o3].reshape(b, s, N_KV_HEADS, HEAD_DIM)
    u = z[..., o3:o4]
    gate_attn = jax.nn.sigmoid(z[..., o4:o5])
    gate_pool = jax.nn.sigmoid(z[..., o5:])

    q = apply_axial_rope(rms_norm(q, q_norm), cos, sin)
    k = apply_axial_rope(rms_norm(k, k_norm), cos, sin)
    attn = block_attention(q, k, v)
    pool = multiscale_pool(u, w_pool, pool_scale)

    merged = gate_attn * attn + gate_pool * pool
    x = x + rms_norm(merged @ w_out, norm_mix_post)

    h = rms_norm(x, norm_ffn_pre)
    f = (jax.nn.silu(h @ w_ffn_gate) * (h @ w_ffn_up)) @ w_ffn_down
    x = x + rms_norm(f, norm_ffn_post)

    e = p_i @ w_ple_in
    g = jax.nn.sigmoid(x @ w_ple_gate)
    x = x + rms_norm(g * e, norm_ple)
    return x


def setup_inputs(seed: int = 0) -> dict:
    key = jax.random.key(seed)
    ks = jax.random.split(key, 20)
    f32 = jnp.float32

    def w(k, shape, fan_in):
        return jax.random.normal(k, shape, f32) * (fan_in ** -0.5)

    def gain(k, shape):
        return 1.0 + 0.1 * jax.random.normal(k, shape, f32)

    return {
        "x": jax.random.normal(ks[0], (BATCH, SEQ, D_MODEL), f32),
        "p": jax.random.normal(ks[1], (DEPTH, BATCH, SEQ, D_PLE), f32),
        "w_in": w(ks[2], (DEPTH, D_MODEL, IN_WIDTH), D_MODEL),
        "q_norm": gain(ks[3], (DEPTH, HEAD_DIM)),
        "k_norm": gain(ks[4], (DEPTH, HEAD_DIM)),
        "w_pool": w(ks[5], (DEPTH, N_POOL_GROUPS, POOL_GROUP_IN, POOL_GROUP_OUT), POOL_GROUP_IN),
        "pool_scale": gain(ks[6], (DEPTH, D_MODEL)),
        "w_out": w(ks[7], (DEPTH, D_MODEL, D_MODEL), D_MODEL),
        "norm_mix_pre": gain(ks[8], (DEPTH, D_MODEL)),
        "norm_mix_post": gain(ks[9], (DEPTH, D_MODEL)),
        "norm_ffn_pre": gain(ks[10], (DEPTH, D_MODEL)),
        "norm_ffn_post": gain(ks[11], (DEPTH, D_MODEL)),
        "w_ffn_gate": w(ks[12], (DEPTH, D_MODEL, D_FF), D_MODEL),
        "w_ffn_up": w(ks[13], (DEPTH, D_MODEL, D_FF), D_MODEL),
        "w_ffn_down": w(ks[14], (DEPTH, D_FF, D_MODEL), D_FF),
        "w_ple_in": w(ks[15], (DEPTH, D_PLE, D_MODEL), D_PLE),
        "w_ple_gate": w(ks[16], (DEPTH, D_MODEL, D_MODEL), D_MODEL),
        "norm_ple": gain(ks[17], (DEPTH, D_MODEL)),
    }


def reference(x, p, w_in, q_norm, k_norm, w_pool, pool_scale, w_out,
              norm_mix_pre, norm_mix_post, norm_ffn_pre, norm_ffn_post,
              w_ffn_gate, w_ffn_up, w_ffn_down, w_ple_in, w_ple_gate, norm_ple):
    seq = x.shape[1]
    cos, sin = axial_rope_tables(seq)
    h = x
    for i in range(DEPTH):
        h = hybrid_layer(h, p[i], w_in[i], q_norm[i], k_norm[i], w_pool[i], pool_scale[i], w_out[i],
                         norm_mix_pre[i], norm_mix_post[i], norm_ffn_pre[i], norm_ffn_post[i],
                         w_ffn_gate[i], w_ffn_up[i], w_ffn_down[i],
                         w_ple_in[i], w_ple_gate[i], norm_ple[i], cos, sin)
    return h
```

```python
import contextlib
import numpy as np
import ml_dtypes
import concourse.bass as bass
import concourse.mybir as mybir
from concourse.bass_utils import run_bass_kernel_spmd

F32 = mybir.dt.float32
BF16 = mybir.dt.bfloat16
AF = mybir.ActivationFunctionType
ALU = mybir.AluOpType

NCORES = 8
D = 4096
KC = 32
SEQ = 8192
TOK = SEQ // NCORES
T = 512
NT = TOK // T
DFF = 11008
FFC = DFF // 128
O1, O2, O3, O4, O5 = 4096, 5120, 6144, 8192, 12288
EPS = 1e-6
PAGE = 512
DBG = set()

C_MIXPRE, C_MIXPOST, C_FFNPRE, C_FFNPOST, C_PLE, C_PSCALE, C_QN, C_KN = 0, 32, 64, 96, 128, 160, 192, 193
CL = 194


class TV:
    __slots__ = ("ap", "res")

    def __init__(self, ap, res=()):
        self.ap = ap
        self.res = tuple(res)


class Buf:
    def __init__(self, arena, off, shape, dtype):
        self.off = off
        self.shape = tuple(shape)
        self.dtype = dtype
        self.esz = 4 if dtype == F32 else 2
        self.n = int(np.prod(shape))
        self.nbytes = self.n * self.esz
        assert off % 4 == 0
        base = arena[:, off // 2: (off + self.nbytes) // 2]
        if dtype == F32:
            base = base.bitcast(F32)
        self.flat = base
        if len(shape) == 2:
            base = base.rearrange("p (a b) -> p a b", b=shape[1])
        elif len(shape) == 3:
            base = base.rearrange("p (a b c) -> p a b c", b=shape[1], c=shape[2])
        self.ap = base

    def pages(self, lo_e, hi_e):
        lo = self.off + lo_e * self.esz
        hi = self.off + hi_e * self.esz
        return [("sb", p) for p in range(lo // PAGE, (hi - 1) // PAGE + 1)]

    def all(self):
        return TV(self.ap, self.pages(0, self.n))

    def c(self, i, a=None, b=None):
        n1 = self.shape[1]
        if a is None:
            a, b = 0, n1
        return TV(self.ap[:, i, a:b], self.pages(i * n1 + a, i * n1 + b))

    def cs(self, i0, i1):
        n1 = self.shape[1]
        return TV(self.ap[:, i0:i1, :], self.pages(i0 * n1, i1 * n1))

    def s(self, a, b):
        return TV(self.flat[:, a:b], self.pages(a, b))

    def v(self, ap, lo_e, hi_e):
        return TV(ap, self.pages(lo_e, hi_e))


class Op:
    __slots__ = ("eng", "fn", "deps", "dma", "signal", "val", "sem", "gidx", "inc")


ENGS = ("pe", "act", "dve", "pool", "sp")


class Sched:
    def __init__(self):
        self.streams = {e: [] for e in ENGS}
        self.lastw = {}
        self.readers = {}
        self.dma_last = {}
        self.dma_keys = []
        self.nops = 0
        self.finals = []

    def op(self, eng, fn, reads=(), writes=(), dma=None, final=False, inc=16):
        o = Op()
        o.inc = inc
        o.eng = eng
        o.fn = fn
        o.dma = dma
        o.signal = dma is not None
        o.val = None
        o.sem = None
        o.gidx = self.nops
        self.nops += 1
        deps = {}
        psr = [t for t in reads if t.res and t.res[0][0] == "ps"]
        if psr:
            reads = [t for t in reads if not (t.res and t.res[0][0] == "ps")]
            writes = list(writes) + psr
        for t in reads:
            for r in t.res:
                w = self.lastw.get(r)
                if w is not None:
                    deps[id(w)] = w
        for t in writes:
            for r in t.res:
                w = self.lastw.get(r)
                if w is not None:
                    deps[id(w)] = w
                rd = self.readers.get(r)
                if rd:
                    for x in rd.values():
                        deps[id(x)] = x
        if dma is not None:
            p = self.dma_last.get(dma)
            if p is not None:
                deps[id(p)] = p
            else:
                self.dma_keys.append(dma)
            self.dma_last[dma] = o
        dl = []
        for d in deps.values():
            if d is o:
                continue
            if d.eng == "pe" and eng == "pe" and d.dma is None and dma is None:
                continue
            dl.append(d)
        o.deps = dl
        rkey = eng if dma is None else ("dma", dma)
        for t in reads:
            for r in t.res:
                rd = self.readers.get(r)
                if rd is None:
                    rd = {}
                    self.readers[r] = rd
                rd[rkey] = o
        for t in writes:
            for r in t.res:
                self.lastw[r] = o
                self.readers[r] = {}
        self.streams[eng].append(o)
        if final:
            self.finals.append(o)
        return o

    def emit(self, nc):
        for e in ENGS:
            for o in self.streams[e]:
                for d in o.deps:
                    d.signal = True
        for o in self.finals:
            o.signal = True
        with contextlib.ExitStack() as st:
            esem = {e: st.enter_context(nc.semaphore("sem_" + e)) for e in ENGS}
            dsem = {k: st.enter_context(nc.semaphore("dsem_%d" % i)) for i, k in enumerate(self.dma_keys)}
            dcount = {k: 0 for k in self.dma_keys}
            allops = sorted((o for e in ENGS for o in self.streams[e]), key=lambda o: o.gidx)
            ecount = {e: 0 for e in ENGS}
            for o in allops:
                if o.dma is not None:
                    dcount[o.dma] += o.inc
                    o.sem = dsem[o.dma]
                    o.val = dcount[o.dma]
                elif o.signal:
                    ecount[o.eng] += 1
                    o.sem = esem[o.eng]
                    o.val = ecount[o.eng]
            block = st.enter_context(nc.Block())
            engobj = {"pe": "tensor", "act": "scalar", "dve": "vector", "pool": "gpsimd", "sp": "sync"}
            finals = self.finals

            def run(e, eng):
                waited = {}

                def wait(d):
                    k = id(d.sem)
                    if waited.get(k, 0) >= d.val:
                        return
                    eng.wait_ge(d.sem, d.val)
                    waited[k] = d.val

                for o in self.streams[e]:
                    for d in o.deps:
                        wait(d)
                    ins = o.fn(eng)
                    if o.dma is not None:
                        ins.then_inc(o.sem, o.inc)
                    elif o.signal:
                        ins.then_inc(o.sem, 1)
                if e == "sp":
                    for d in finals:
                        wait(d)

            for e in ENGS:
                if not self.streams[e] and e != "sp":
                    continue
                getattr(block, engobj[e])(lambda eng, e=e: run(e, eng))


class WStream:
    NSLOT = 4
    SLOT_E = 4096

    def __init__(self, S, arena, off, plan=None):
        self.S = S
        self.recording = plan is None
        self.plan = [] if plan is None else plan
        self.cur = 0
        self.issued = 0
        self.slots = [Buf(arena, off + i * self.SLOT_E * 2, [self.SLOT_E], BF16) for i in range(self.NSLOT)]

    def _view(self, k):
        src, kcn, ncols, res = self.plan[k]
        b = self.slots[k % self.NSLOT]
        ap = b.flat[:, 0:kcn * ncols].rearrange("p (a b) -> p a b", b=ncols)
        return TV(ap, b.pages(0, kcn * ncols))

    def _issue(self, k):
        src, kcn, ncols, res = self.plan[k]
        tv = self._view(k)
        self.S.op("pool", lambda e, tv=tv, src=src: e.dma_start(out=tv.ap, in_=src), reads=([res] if res is not None else []), writes=[tv],
                  dma=("w", k % self.NSLOT))

    def next(self, src, kcn, ncols, res=None):
        assert kcn * ncols <= self.SLOT_E
        i = self.cur
        if self.recording:
            self.plan.append((src, kcn, ncols, res))
        else:
            assert self.plan[i][1:3] == (kcn, ncols)
        lim = len(self.plan) if not self.recording else i + 1
        while self.issued < min(lim, i + self.NSLOT):
            self._issue(self.issued)
            self.issued += 1
        self.cur += 1
        return self._view(i)


class Prog:
    def __init__(self, mode, layer=None):
        self.mode = mode
        self.layer = layer
        self.nc = bass.Bass("TRN2", target_bir_lowering=False)
        self.dram = {}
        self.in_names = []
        self.out_names = []

    def dt(self, name, shape, dtype, role, shared=False):
        if name in self.dram:
            return self.dram[name]
        kind = {"in": "ExternalInput", "out": "ExternalOutput", "scratch": "Internal"}[role]
        if shared:
            h = self.nc.dram_tensor(name, list(shape), dtype, kind=kind, addr_space="Shared")
        else:
            h = self.nc.dram_tensor(name, list(shape), dtype, kind=kind)
        self.dram[name] = (h.ap(), role)
        if role == "in":
            self.in_names.append(name)
        if role == "out":
            self.out_names.append(name)
        return self.dram[name]

    def build(self):
        nc = self.nc
        mode = self.mode
        layers = [0, 1] if mode == "F" else [self.layer]

        def role_x(l):
            if mode == "F":
                return "in" if l == 0 else "scratch"
            return "in"

        def role_xout(l):
            if mode == "F":
                return "out" if l == 1 else "scratch"
            return "out"

        xT = {}
        for l in layers:
            xT[l] = self.dt("xT%d" % l, [D, TOK], F32, role_x(l))[0]
        for l in layers:
            if mode in ("B", "F"):
                if mode == "F" and l == 0:
                    xT[1] = self.dt("xT1", [D, TOK], F32, "scratch")[0]
                else:
                    xT[l + 1] = self.dt("xT%d" % (l + 1), [D, TOK], F32, role_xout(l))[0]
        cst = self.dt("cst", [128, 2 * CL], F32, "in")[0]
        cosd = self.dt("cosT", [128, TOK], F32, "in")[0]
        sind = self.dt("sinT", [128, TOK], F32, "in")[0]
        rtd = self.dt("rotT", [128, 128], F32, "in")[0]
        W = {}
        kvsrc, kvall, xhsrc, xhall = {}, {}, {}, {}
        for l in layers:
            if mode == "A":
                W["in", l] = self.dt("w_kv%d" % l, [D, 2048], F32, "in")[0]
                kvsrc[l] = self.dt("kvsrc%d" % l, [2048, TOK], BF16, "out")[0]
                xhsrc[l] = self.dt("xhsrc%d" % l, [128, 1024], F32, "out")[0]
            else:
                self.wshard = getattr(self, "wshard", {})
                for key, nm, rows, cols in (("in", "w_in", D, 16384), ("pool", "w_pool", 2048, 1024), ("out", "w_out", D, D),
                                            ("gate", "w_gate", D, DFF), ("up", "w_up", D, DFF), ("down", "w_down", DFF, D),
                                            ("plein", "w_plein", 256, D), ("plegate", "w_plegate", D, D)):
                    if mode == "F":
                        ws = self.dt("%s%d" % (nm, l), [rows // NCORES, cols], F32, "in")[0]
                        wi = self.dt("wi_%s%d" % (nm, l), [rows // NCORES, cols], F32, "scratch")[0]
                        wf = self.dt("wf_%s%d" % (nm, l), [rows, cols], F32, "scratch", shared=True)[0]
                        self.wshard[key, l] = (ws, wi, wf, rows // NCORES)
                    else:
                        wf = self.dt("%s%d" % (nm, l), [rows, cols], F32, "in")[0]
                    W[key, l] = wf.rearrange("(g k) n -> g k n", g=4) if key == "pool" else wf
                W["p", l] = self.dt("pT%d" % l, [256, TOK], F32, "in")[0]
                if mode == "B":
                    kvall[l] = self.dt("kvall%d" % l, [NCORES * 2048, TOK], BF16, "in")[0]
                    xhall[l] = self.dt("xhall%d" % l, [NCORES * 128, 1024], F32, "in")[0]
                else:
                    kvsrc[l] = self.dt("kvsrc%d" % l, [2048, TOK], BF16, "scratch")[0]
                    xhsrc[l] = self.dt("xhsrc%d" % l, [128, 1024], F32, "scratch")[0]
                    kvall[l] = self.dt("kvall%d" % l, [NCORES * 2048, TOK], BF16, "scratch", shared=True)[0]
                    xhall[l] = self.dt("xhall%d" % l, [NCORES * 128, 1024], F32, "scratch", shared=True)[0]
        if mode in ("B", "F"):
            invcd = self.dt("invc", [128, 4, TOK], F32, "in")[0]
            seld = self.dt("sel", [128, 24], F32, "in")[0]
            xmid = self.dt("xmid", [D, TOK], F32, "scratch")[0]
            xmid2 = self.dt("xmid2", [D, TOK], F32, "scratch")[0]

        with contextlib.ExitStack() as st:
            arena = st.enter_context(nc.sbuf_tensor("arena", [128, 103424], BF16))
            psum = [st.enter_context(nc.psum_tensor("ps%d" % i, [128, 512], F32)) for i in range(8)]
            PS = [TV(psum[i][:], [("ps", i)]) for i in range(8)]

            off = [0]

            def alloc(shape, dtype):
                b = Buf(arena, off[0], shape, dtype)
                off[0] += (b.nbytes + PAGE - 1) // PAGE * PAGE
                return b

            wb_off = off[0]
            off[0] += WStream.NSLOT * WStream.SLOT_E * 2
            HT = alloc([KC, 528], BF16)
            big_off = off[0]
            BIG = alloc([KC, T], F32)
            MG = alloc([KC, T], BF16)
            FT = [Buf(arena, MG.off + i * 8 * T * 2, [8, T], BF16) for i in range(2)]
            CST = alloc([2 * CL], F32)
            COS = alloc([T], F32)
            SIN = alloc([T], F32)
            INVC = alloc([T], F32)
            SEL = alloc([24], F32)
            RT = alloc([128], F32)
            ONESF = alloc([128], F32)
            ONESB = alloc([128], BF16)
            EPSB = alloc([1], F32)
            SQ = [alloc([T], F32) for _ in range(2)]
            RS = alloc([T], F32)
            RSH = alloc([16], F32)
            XST = [alloc([T], F32) for _ in range(2)]
            PT = [alloc([T], BF16) for _ in range(4)]
            QT = alloc([2, T], BF16)
            PB = Buf(arena, QT.off, [2, T], BF16)
            XH = alloc([2 * KC, 16], F32)
            KO = [alloc([T], BF16) for _ in range(2)]
            XE = alloc([2, KC, 16], F32)
            VO = [alloc([256], BF16) for _ in range(2)]
            assert off[0] <= 103424 * 2, off[0]
            KT = Buf(arena, big_off, [SEQ], BF16)
            VV = Buf(arena, big_off + 16384, [64, 128], BF16)
            so = [big_off + 32768]

            def salloc(shape, dtype):
                b = Buf(arena, so[0], shape, dtype)
                so[0] += (b.nbytes + PAGE - 1) // PAGE * PAGE
                assert so[0] <= big_off + 65536
                return b

            QS = [salloc([T], F32) for _ in range(2)]
            QN = [salloc([T], F32) for _ in range(2)]
            T1 = salloc([T], F32)
            GA = salloc([2, T], F32)
            GP = salloc([2, T], F32)
            UE = salloc([528], F32)
            WA = salloc([528], F32)
            WBb = salloc([528], F32)
            DL = salloc([4, T], BF16)
            RL = salloc([T], F32)
            XSTG = Buf(arena, big_off, [NCORES * 2, KC, 16], F32)

            def run_pass(S, Wst):
                self._emit_all(S, Wst, locals_=dict(
                    nc=nc, layers=layers, xT=xT, cst=cst, cosd=cosd, sind=sind, rtd=rtd, W=W,
                    kvsrc=kvsrc, kvall=kvall, xhsrc=xhsrc, xhall=xhall,
                    invcd=invcd if mode != "A" else None, seld=seld if mode != "A" else None,
                    xmid=xmid if mode != "A" else None, xmid2=xmid2 if mode != "A" else None,
                    PS=PS, HT=HT, BIG=BIG, MG=MG, FT=FT, CST=CST, COS=COS, SIN=SIN, INVC=INVC, SEL=SEL, RT=RT,
                    ONESF=ONESF, ONESB=ONESB, EPSB=EPSB, SQ=SQ, RS=RS, RSH=RSH, XST=XST, PT=PT, QT=QT, PB=PB,
                    XH=XH, KO=KO, VO=VO, XE=XE, KT=KT, VV=VV, QS=QS, QN=QN, T1=T1, GA=GA, GP=GP, UE=UE, WA=WA,
                    WBb=WBb, DL=DL, RL=RL, XSTG=XSTG))

            S0 = Sched()
            W0 = WStream(S0, arena, wb_off)
            run_pass(S0, W0)
            S1 = Sched()
            W1 = WStream(S1, arena, wb_off, plan=W0.plan)
            run_pass(S1, W1)
            assert W1.cur == len(W0.plan)
            self.nops = S1.nops
            S1.emit(nc)
        return nc

    def _emit_all(self, S, Wst, locals_):
        L = locals_
        mode = self.mode
        nc = L["nc"]
        PS, HT, BIG, MG, FT, CST = L["PS"], L["HT"], L["BIG"], L["MG"], L["FT"], L["CST"]
        COS, SIN, INVC, SEL, RT = L["COS"], L["SIN"], L["INVC"], L["SEL"], L["RT"]
        ONESF, ONESB, EPSB, SQ, RS, RSH = L["ONESF"], L["ONESB"], L["EPSB"], L["SQ"], L["RS"], L["RSH"]
        XST, PT, QT, PB, XH, KO, VO = L["XST"], L["PT"], L["QT"], L["PB"], L["XH"], L["KO"], L["VO"]
        XE = L["XE"]
        KT, VV, QS, QN, T1, GA, GP = L["KT"], L["VV"], L["QS"], L["QN"], L["T1"], L["GA"], L["GP"]
        UE, WA, WBb, DL, RL, XSTG = L["UE"], L["WA"], L["WBb"], L["DL"], L["RL"], L["XSTG"]
        xT, W = L["xT"], L["W"]
        kvsrc, kvall, xhsrc, xhall = L["kvsrc"], L["kvall"], L["xhsrc"], L["xhall"]
        xmid, xmid2 = L["xmid"], L["xmid2"]

        cnt = {"gb": 0, "sq": 0, "xst": 0, "pt": 0, "ko": 0, "vo": 0}

        def rot(name, n):
            v = cnt[name]
            cnt[name] = (v + 1) % n
            return v

        def gbank():
            return PS[rot("gb", 4)]

        def is_out(ap_role):
            return ap_role == "out"

        def dres(name, *idx):
            return TV(None, [("d", name) + tuple(idx)])

        def role_of(ap):
            for n, (a, r) in self.dram.items():
                if a is ap:
                    return r
            return None

        S.op("sp", lambda e: e.dma_start(out=CST.all().ap, in_=L["cst"]), writes=[CST.all()], dma="c0")
        S.op("sp", lambda e: e.dma_start(out=RT.all().ap, in_=L["rtd"]), writes=[RT.all()], dma="c1")
        if mode != "A":
            S.op("sp", lambda e: e.dma_start(out=SEL.all().ap, in_=L["seld"]), writes=[SEL.all()], dma="c2")
        S.op("dve", lambda e: e.memset(ONESF.all().ap, 1.0), writes=[ONESF.all()])
        S.op("dve", lambda e: e.memset(ONESB.all().ap, 1.0), writes=[ONESB.all()])
        S.op("dve", lambda e: e.memset(EPSB.all().ap, EPS), writes=[EPSB.all()])

        def gcol(l, c):
            return CST.flat[:, l * CL + c: l * CL + c + 1]

        def rstd_from(ps, n, out):
            S.op("act", lambda e: e.activation(out=out.ap, in_=ps.ap, func=AF.Sqrt, bias=EPSB.flat[:, 0:1], scale=1.0 / n),
                 reads=[ps, EPSB.all()], writes=[out])
            S.op("dve", lambda e: e.reciprocal(out=out.ap, in_=out.ap), reads=[out], writes=[out])

        def load_x_tile(src, tt, sres=None):
            xv = src.rearrange("(c p) t -> p c t", p=128)
            for g in range(8):
                dst = BIG.cs(4 * g, 4 * g + 4)
                rd = [sres(tt, g)] if sres else []
                S.op("sp", lambda e, g=g, dst=dst: e.dma_start(out=dst.ap, in_=xv[:, 4 * g:4 * g + 4, tt * T:(tt + 1) * T]),
                     reads=rd, writes=[dst], dma=("xl", g))

        def sumsq_chunk(src_tv, c, n):
            sq = SQ[rot("sq", 2)].all()
            S.op("act", lambda e: e.activation(out=sq.ap, in_=src_tv.ap, func=AF.Square), reads=[src_tv], writes=[sq])
            S.op("pe", lambda e: e.matmul(PS[7].ap, lhsT=ONESF.all().ap, rhs=sq.ap, start=(c == 0), stop=(c == n - 1)),
                 reads=[ONESF.all(), sq], writes=[PS[7]])

        def norm_from_big(l, gc, halo_tile=None):
            for c in range(KC):
                sumsq_chunk(BIG.c(c), c, KC)
            rstd_from(PS[7], D, RS.all())
            for c in range(KC):
                S.op("dve", lambda e, c=c: e.scalar_tensor_tensor(out=HT.c(c, 0, T).ap, in0=BIG.c(c).ap, scalar=gcol(l, gc + c),
                                                                   in1=RS.all().ap, op0=ALU.mult, op1=ALU.mult),
                     reads=[BIG.c(c), CST.all(), RS.all()], writes=[HT.c(c, 0, T)])
            if halo_tile is not None:
                tt = halo_tile
                hb = gbank()
                hps = TV(hb.ap[:, 0:16], hb.res)
                for c in range(KC):
                    xh = XH.c(tt * KC + c)
                    sq = SQ[rot("sq", 2)]
                    sqv = sq.s(0, 16)
                    S.op("act", lambda e, xh=xh, sqv=sqv: e.activation(out=sqv.ap, in_=xh.ap, func=AF.Square), reads=[xh], writes=[sqv])
                    S.op("pe", lambda e, c=c, sqv=sqv: e.matmul(hps.ap, lhsT=ONESF.all().ap, rhs=sqv.ap, start=(c == 0), stop=(c == KC - 1)),
                         reads=[ONESF.all(), sqv], writes=[hps])
                rstd_from(hps, D, RSH.all())
                for c in range(KC):
                    xh = XH.c(tt * KC + c)
                    S.op("dve", lambda e, c=c, xh=xh: e.scalar_tensor_tensor(out=HT.c(c, T, T + 16).ap, in0=xh.ap, scalar=gcol(l, gc + c),
                                                                              in1=RSH.all().ap, op0=ALU.mult, op1=ALU.mult),
                         reads=[xh, CST.all(), RSH.all()], writes=[HT.c(c, T, T + 16)])

        def wres(wkey):
            return dres("wf", wkey) if (mode == "F" and wkey is not None) else None

        def gemm_fm_block(wsrc, col0, rhs_fn, kcs=KC, halo_fn=None, wrow0=0, wkey=None):
            wv = wsrc.rearrange("(c p) n -> p c n", p=128)
            banks = [gbank(), gbank()]
            hbanks = None
            if halo_fn is not None:
                hbanks = [gbank(), gbank()]
                hbanks = [TV(b.ap[:, 0:16], b.res) for b in hbanks]
            half = 16 if kcs > 16 else kcs
            for h0 in range(0, kcs, half):
                kn = min(half, kcs - h0)
                slot = Wst.next(wv[:, wrow0 + h0: wrow0 + h0 + kn, col0:col0 + 256], kn, 256, wres(wkey))
                for kl in range(kn):
                    kc = h0 + kl
                    for j in range(2):
                        rhs = rhs_fn(kc)
                        S.op("pe", lambda e, j=j, kl=kl, kc=kc, rhs=rhs, slot=slot: e.matmul(
                            banks[j].ap, lhsT=slot.ap[:, kl, j * 128:(j + 1) * 128], rhs=rhs.ap, start=(kc == 0), stop=(kc == kcs - 1)),
                            reads=[slot, rhs], writes=[banks[j]])
                        if halo_fn is not None:
                            rh = halo_fn(kc)
                            S.op("pe", lambda e, j=j, kl=kl, kc=kc, rh=rh, slot=slot: e.matmul(
                                hbanks[j].ap, lhsT=slot.ap[:, kl, j * 128:(j + 1) * 128], rhs=rh.ap, start=(kc == 0), stop=(kc == kcs - 1)),
                                reads=[slot, rh], writes=[hbanks[j]])
            return banks, hbanks

        def ht_main(kc):
            return HT.c(kc, 0, T)

        def ht_halo(kc):
            return HT.c(kc, T, T + 16)

        def qk_stage1(ps, i):
            qs = QS[i].all()
            sq = SQ[rot("sq", 2)].all()
            S.op("act", lambda e: e.activation(out=sq.ap, in_=ps.ap, func=AF.Square), reads=[ps], writes=[sq])
            S.op("dve", lambda e: e.tensor_copy(out=qs.ap, in_=ps.ap), reads=[ps], writes=[qs])
            return sq

        def qk_stage2(sq, i, l, gc):
            b = gbank()
            S.op("pe", lambda e: e.matmul(b.ap, lhsT=ONESF.all().ap, rhs=sq.ap, start=True, stop=True),
                 reads=[ONESF.all(), sq], writes=[b])
            rstd_from(b, 128, RL.all())
            qn = QN[i].all()
            S.op("dve", lambda e: e.scalar_tensor_tensor(out=qn.ap, in0=QS[i].all().ap, scalar=gcol(l, gc), in1=RL.all().ap,
                                                         op0=ALU.mult, op1=ALU.mult),
                 reads=[QS[i].all(), CST.all(), RL.all()], writes=[qn])

        def qk_stage3(i, out_tv):
            qn = QN[i].all()
            b = gbank()
            S.op("pe", lambda e: e.matmul(b.ap, lhsT=RT.all().ap, rhs=qn.ap, start=True, stop=True),
                 reads=[RT.all(), qn], writes=[b])
            t1 = T1.all()
            S.op("dve", lambda e: e.tensor_tensor(out=t1.ap, in0=b.ap, in1=SIN.all().ap, op=ALU.mult), reads=[b, SIN.all()], writes=[t1])
            S.op("dve", lambda e: e.tensor_tensor(out=qn.ap, in0=qn.ap, in1=COS.all().ap, op=ALU.mult), reads=[qn, COS.all()], writes=[qn])
            S.op("dve", lambda e: e.tensor_tensor(out=out_tv.ap, in0=qn.ap, in1=t1.ap, op=ALU.add), reads=[qn, t1], writes=[out_tv])

        def load_tables(tt):
            S.op("sp", lambda e: e.dma_start(out=COS.all().ap, in_=L["cosd"][:, tt * T:(tt + 1) * T]), writes=[COS.all()], dma="cos")
            S.op("sp", lambda e: e.dma_start(out=SIN.all().ap, in_=L["sind"][:, tt * T:(tt + 1) * T]), writes=[SIN.all()], dma="sin")

        def phase_A(l):
            xsrc = xT[l]
            win = W["in", l]
            kcol0 = 0 if mode == "A" else O1
            vcol0 = 1024 if mode == "A" else O2
            xres = (lambda tt, g: dres("x", l, tt, g)) if role_of(xsrc) != "in" else None
            fin = role_of(kvsrc[l]) == "out"
            kpart = kvsrc[l][0:1024, :]
            vpart = kvsrc[l][1024:2048, :].rearrange("(h a) (b d) -> h (a b) d", h=8, d=128)
            for tt in range(NT):
                load_tables(tt)
                load_x_tile(xsrc, tt, xres)
                big3 = BIG.ap
                for (src0, slot, col0) in (((0, 8), (0 if tt == 0 else 1), (0 if tt == 0 else 8)), ((T - 8, T), (1 if tt == 0 else 0), (0 if tt == 0 else 8))):
                    dstv = TV(XE.ap[:, slot, :, col0:col0 + 8], XE.pages(slot * KC * 16, (slot + 1) * KC * 16))
                    srcv = TV(big3[:, :, src0[0]:src0[1]], BIG.pages(0, KC * T))
                    S.op("act", lambda e, dstv=dstv, srcv=srcv: e.activation(out=dstv.ap, in_=srcv.ap, func=AF.Copy), reads=[srcv], writes=[dstv])
                if tt == NT - 1:
                    S.op("sp", lambda e: e.dma_start(out=xhsrc[l], in_=XE.flat), reads=[XE.all()], writes=[dres("xhsrc", l)], dma="xe2", final=fin)
                if "noNorm" not in DBG:
                    norm_from_big(l, C_MIXPRE)
                for blk in range(0 if "noK" in DBG else 4):
                    banks, _ = gemm_fm_block(win, kcol0 + blk * 256, ht_main, wkey=("in", l))
                    if "noQK" in DBG:
                        continue
                    sqs = [qk_stage1(banks[i], i) for i in range(2)]
                    if "noQK2" in DBG:
                        continue
                    for i in range(2):
                        qk_stage2(sqs[i], i, l, C_KN)
                    if "noQK3" in DBG:
                        continue
                    for i in range(2):
                        ko = KO[rot("ko", 2)]
                        qk_stage3(i, ko.all())
                        j = blk * 2 + i
                        S.op("sp", lambda e, ko=ko, j=j, tt=tt: e.dma_start(out=kpart[j * 128:(j + 1) * 128, tt * T:(tt + 1) * T], in_=ko.all().ap),
                             reads=[ko.all()], writes=[dres("kvsrc", l)], dma=("ko", j % 2), final=fin)
                wv = win.rearrange("(c p) n -> p c n", p=128)
                for blk in range(0 if "noV" in DBG else 4):
                    banks = [gbank() for _ in range(4)]
                    bv = [TV(b.ap[:, 0:256], b.res) for b in banks]
                    for h0 in (0, 16):
                        slot = Wst.next(wv[:, h0:h0 + 16, vcol0 + blk * 256: vcol0 + blk * 256 + 256], 16, 256, wres(("in", l)))
                        for kl in range(16):
                            kc = h0 + kl
                            for tc in range(4):
                                lh = HT.c(kc, tc * 128, tc * 128 + 128)
                                S.op("pe", lambda e, tc=tc, kl=kl, kc=kc, lh=lh, slot=slot, bv=bv: e.matmul(
                                    bv[tc].ap, lhsT=lh.ap, rhs=slot.ap[:, kl, :], start=(kc == 0), stop=(kc == KC - 1)),
                                    reads=[slot, lh], writes=[bv[tc]])
                    for tc in range(4):
                        vo = VO[rot("vo", 2)]
                        S.op("act", lambda e, tc=tc, vo=vo, bv=bv: e.activation(out=vo.all().ap, in_=bv[tc].ap, func=AF.Copy), reads=[bv[tc]], writes=[vo.all()])
                        t0 = tt * T + tc * 128
                        dst = vpart[2 * blk:2 * blk + 2, t0:t0 + 128, :].rearrange("h t d -> t h d")
                        S.op("sp", lambda e, vo=vo, dst=dst: e.dma_start(out=dst, in_=vo.all().ap.rearrange("p (h d) -> p h d", d=128)),
                             reads=[vo.all()], writes=[dres("kvsrc", l)], dma=("vo", tc % 2), final=fin)

        def phase_gather(l):
            grp = [list(range(NCORES))]
            S.op("pool", lambda e: e.collective_compute("AllGather", ALU.bypass, replica_groups=grp, ins=[kvsrc[l][:, :]], outs=[kvall[l][:, :]]),
                 reads=[dres("kvsrc", l)], writes=[dres("kvall", l)], dma=("cc", 0), inc=1)
            S.op("pool", lambda e: e.collective_compute("AllGather", ALU.bypass, replica_groups=grp, ins=[xhsrc[l][:, :]], outs=[xhall[l][:, :]]),
                 reads=[dres("xhsrc", l)], writes=[dres("xhall", l)], dma=("cc", 1), inc=1)

        def postnorm_residual(l, gc, xsrc, xsrc_res, after_chunk):
            rstd_from(PS[7], D, RS.all())
            xv = xsrc.rearrange("(c p) t -> p c t", p=128)
            for c in range(KC):
                xs = XST[rot("xst", 2)].all()
                tt = cur["tt"]
                rd = [xsrc_res(tt, c // 4)] if xsrc_res else []
                S.op("sp", lambda e, c=c, xs=xs, tt=tt: e.dma_start(out=xs.ap, in_=xv[:, c, tt * T:(tt + 1) * T]), reads=rd, writes=[xs],
                     dma=("xs", cnt["xst"]))
                S.op("dve", lambda e, c=c: e.scalar_tensor_tensor(out=BIG.c(c).ap, in0=BIG.c(c).ap, scalar=gcol(l, gc + c), in1=RS.all().ap,
                                                                   op0=ALU.mult, op1=ALU.mult),
                     reads=[BIG.c(c), CST.all(), RS.all()], writes=[BIG.c(c)])
                S.op("dve", lambda e, c=c, xs=xs: e.tensor_tensor(out=BIG.c(c).ap, in0=BIG.c(c).ap, in1=xs.ap, op=ALU.add),
                     reads=[BIG.c(c), xs], writes=[BIG.c(c)])
                after_chunk(c)

        def store_big(dst, dname, l, tt, final=False):
            dv = dst.rearrange("(c p) t -> p c t", p=128)
            for g in range(8):
                src = BIG.cs(4 * g, 4 * g + 4)
                S.op("sp", lambda e, g=g, src=src: e.dma_start(out=dv[:, 4 * g:4 * g + 4, tt * T:(tt + 1) * T], in_=src.ap),
                     reads=[src], writes=[dres(dname, l, tt, g)], dma=("st", g % 4), final=final)

        cur = {"tt": 0}

        def phase_B(l):
            xsrc = xT[l]
            xdst = xT[l + 1]
            win = W["in", l]
            xres = (lambda tt, g: dres("x", l, tt, g)) if role_of(xsrc) != "in" else None
            xv = xsrc.rearrange("(c p) t -> p c t", p=128)
            kva = kvall[l]
            gather_rd = [dres("kvall", l)] if mode == "F" else []
            hrd = [dres("xhall", l)] if mode == "F" else []
            xha = xhall[l].rearrange("(r p) f -> p r f", p=128)
            dst = TV(XSTG.flat.rearrange("p (r f) -> p r f", r=NCORES), XSTG.pages(0, NCORES * 1024))
            S.op("sp", lambda e, dst=dst: e.dma_start(out=dst.ap, in_=xha), reads=hrd, writes=[dst], dma="xh")
            xh3 = XH.ap
            for (tt, dcol, slot, scol, selc) in ((0, 0, 0, 8, 0), (0, 8, 1, 8, 16), (1, 0, 1, 0, 16), (1, 8, 0, 0, 8)):
                out = TV(xh3[:, tt * KC:(tt + 1) * KC, dcol:dcol + 8], XH.pages(tt * KC * 16, (tt + 1) * KC * 16))
                for r in range(NCORES):
                    src = TV(XSTG.ap[:, r * 2 + slot, :, scol:scol + 8], XSTG.pages((r * 2 + slot) * KC * 16, (r * 2 + slot + 1) * KC * 16))
                    sc = SEL.flat[:, selc + r: selc + r + 1]
                    if r == 0:
                        S.op("dve", lambda e, out=out, src=src, sc=sc: e.tensor_scalar(out=out.ap, in0=src.ap, scalar1=sc, scalar2=None, op0=ALU.mult),
                             reads=[src, SEL.all()], writes=[out])
                    else:
                        S.op("dve", lambda e, out=out, src=src, sc=sc: e.scalar_tensor_tensor(out=out.ap, in0=src.ap, scalar=sc, in1=out.ap,
                                                                                              op0=ALU.mult, op1=ALU.add),
                             reads=[src, SEL.all(), out], writes=[out])

            for tt in range(NT):
                cur["tt"] = tt
                load_tables(tt)
                load_x_tile(xsrc, tt, xres)
                norm_from_big(l, C_MIXPRE, halo_tile=tt)
                for pr in range(16):
                    j = pr // 2
                    g = pr // 4
                    if pr % 2 == 0:
                        ksrc = kva.rearrange("(r x) t -> x r t", r=NCORES)[j * 128:(j + 1) * 128, :, :]
                        kdst = TV(KT.flat.rearrange("p (r t) -> p r t", r=NCORES), KT.pages(0, SEQ))
                        S.op("sp", lambda e, ksrc=ksrc, kdst=kdst: e.dma_start(out=kdst.ap, in_=ksrc), reads=gather_rd, writes=[kdst], dma="kt")
                        vsrc_all = kva.rearrange("(r x) t -> r x t", r=NCORES)[:, 1024:2048, :].rearrange(
                            "r (h a) (b d) -> r h (a b) d", h=8, d=128)
                        for r in range(NCORES):
                            vs = vsrc_all[r, j, :, :].rearrange("(k p) d -> p k d", p=128)
                            vd = TV(VV.ap[:, r * 8:(r + 1) * 8, :], VV.pages(r * 1024, (r + 1) * 1024))
                            S.op("sp", lambda e, vs=vs, vd=vd: e.dma_start(out=vd.ap, in_=vs), reads=gather_rd, writes=[vd], dma=("vv", r % 4))
                    qb, _ = gemm_fm_block(win, pr * 256, ht_main, wkey=("in", l))
                    sqs = [qk_stage1(qb[i], i) for i in range(2)]
                    if pr % 4 == 0:
                        S.op("sp", lambda e, g=g, tt=tt: e.dma_start(out=INVC.all().ap, in_=L["invcd"][:, g, tt * T:(tt + 1) * T]),
                             writes=[INVC.all()], dma="invc")
                        w = (2, 4, 8, 16)[g]
                        for ub in range(2):
                            ubk, uhb = gemm_fm_block(win, O3 + g * 512 + ub * 256, ht_main, halo_fn=ht_halo, wkey=("in", l))
                            for i in range(2):
                                cc = ub * 2 + i
                                S.op("act", lambda e, i=i, ubk=ubk: e.activation(out=UE.s(8, 8 + T).ap, in_=ubk[i].ap, func=AF.Copy),
                                     reads=[ubk[i]], writes=[UE.s(8, 8 + T)])
                                hb = uhb[i]
                                S.op("dve", lambda e, hb=hb: e.tensor_copy(out=UE.s(0, 8).ap, in_=hb.ap[:, 0:8]), reads=[hb], writes=[UE.s(0, 8)])
                                S.op("dve", lambda e, hb=hb: e.tensor_copy(out=UE.s(520, 528).ap, in_=hb.ap[:, 8:16]), reads=[hb], writes=[UE.s(520, 528)])
                                srcb, step, ln = UE, 1, 527
                                bufs = [WA, WBb]
                                k = 0
                                ww = 1
                                while ww < w:
                                    dstb = bufs[k % 2]
                                    S.op("dve", lambda e, srcb=srcb, dstb=dstb, step=step, ln=ln: e.tensor_tensor(
                                        out=dstb.s(0, ln).ap, in0=srcb.s(0, ln).ap, in1=srcb.s(step, step + ln).ap, op=ALU.add),
                                        reads=[srcb.s(0, ln + step)], writes=[dstb.s(0, ln)])
                                    ww *= 2
                                    srcb = dstb
                                    step = ww
                                    ln = ln - ww
                                    k += 1
                                o = 8 - w // 2
                                S.op("dve", lambda e, srcb=srcb, o=o: e.tensor_tensor(out=T1.all().ap, in0=srcb.s(o, o + T).ap, in1=INVC.all().ap, op=ALU.mult),
                                     reads=[srcb.s(o, o + T), INVC.all()], writes=[T1.all()])
                                S.op("dve", lambda e, cc=cc: e.tensor_tensor(out=DL.c(cc).ap, in0=T1.all().ap, in1=UE.s(8, 8 + T).ap, op=ALU.subtract),
                                     reads=[T1.all(), UE.s(8, 8 + T)], writes=[DL.c(cc)])
                    gab, _ = gemm_fm_block(win, O4 + pr * 256, ht_main, wkey=("in", l))
                    for i in range(2):
                        S.op("act", lambda e, i=i, gab=gab: e.activation(out=GA.c(i).ap, in_=gab[i].ap, func=AF.Sigmoid), reads=[gab[i]], writes=[GA.c(i)])
                    for i in range(2):
                        qk_stage2(sqs[i], i, l, C_QN)
                    gpb, _ = gemm_fm_block(win, O5 + pr * 256, ht_main, wkey=("in", l))
                    for i in range(2):
                        S.op("act", lambda e, i=i, gpb=gpb: e.activation(out=GP.c(i).ap, in_=gpb[i].ap, func=AF.Sigmoid), reads=[gpb[i]], writes=[GP.c(i)])
                    for i in range(2):
                        qk_stage3(i, QT.c(i))
                    wp = W["pool", l][g]
                    pb, _ = gemm_fm_block(wp, (pr % 4) * 256, lambda kc: DL.c(kc), kcs=4, wkey=("pool", l))
                    for i in range(2):
                        ch = pr * 2 + i
                        S.op("dve", lambda e, i=i, ch=ch, pb=pb: e.scalar_tensor_tensor(out=GP.c(i).ap, in0=pb[i].ap, scalar=gcol(l, C_PSCALE + ch),
                                                                                      in1=GP.c(i).ap, op0=ALU.mult, op1=ALU.mult),
                             reads=[pb[i], CST.all(), GP.c(i)], writes=[GP.c(i)])
                    for i in range(2):
                        ch = pr * 2 + i
                        q = QT.c(i)
                        ob = PS[6]
                        lb = PS[7]
                        scale = 128 ** -0.5

                        def s_mm(kt, q=q):
                            sb = PS[4 + (kt % 2)]
                            kk = KT.s(kt * 128, (kt + 1) * 128)
                            S.op("pe", lambda e, sb=sb, kk=kk, q=q: e.matmul(sb.ap, lhsT=kk.ap, rhs=q.ap, start=True, stop=True),
                                 reads=[kk, q], writes=[sb])
                            return sb
                        sb_next = s_mm(0)
                        for kt in range(64):
                            sb = sb_next
                            if kt + 1 < 64:
                                sb_next = s_mm(kt + 1)
                            pt = PT[rot("pt", 4)].all()
                            S.op("act", lambda e, sb=sb, pt=pt: e.activation(out=pt.ap, in_=sb.ap, func=AF.Exp, scale=scale), reads=[sb], writes=[pt])
                            vv = TV(VV.ap[:, kt, :], VV.pages(kt * 128, (kt + 1) * 128))
                            S.op("pe", lambda e, kt=kt, vv=vv, pt=pt: e.matmul(ob.ap, lhsT=vv.ap, rhs=pt.ap, start=(kt == 0), stop=(kt == 63)),
                                 reads=[vv, pt], writes=[ob])
                            S.op("pe", lambda e, kt=kt, pt=pt: e.matmul(lb.ap, lhsT=ONESB.all().ap, rhs=pt.ap, start=(kt == 0), stop=(kt == 63)),
                                 reads=[ONESB.all(), pt], writes=[lb])
                        S.op("dve", lambda e: e.reciprocal(out=RL.all().ap, in_=lb.ap), reads=[lb], writes=[RL.all()])
                        S.op("dve", lambda e: e.tensor_tensor(out=T1.all().ap, in0=ob.ap, in1=RL.all().ap, op=ALU.mult), reads=[ob, RL.all()], writes=[T1.all()])
                        S.op("dve", lambda e, i=i: e.tensor_tensor(out=T1.all().ap, in0=T1.all().ap, in1=GA.c(i).ap, op=ALU.mult),
                             reads=[T1.all(), GA.c(i)], writes=[T1.all()])
                        S.op("dve", lambda e, i=i, ch=ch: e.tensor_tensor(out=MG.c(ch).ap, in0=T1.all().ap, in1=GP.c(i).ap, op=ALU.add),
                             reads=[T1.all(), GP.c(i)], writes=[MG.c(ch)])
                for blk in range(16):
                    yb, _ = gemm_fm_block(W["out", l], blk * 256, lambda kc: MG.c(kc), wkey=("out", l))
                    for i in range(2):
                        c = blk * 2 + i
                        S.op("act", lambda e, i=i, c=c, yb=yb: e.activation(out=BIG.c(c).ap, in_=yb[i].ap, func=AF.Copy), reads=[yb[i]], writes=[BIG.c(c)])
                        sumsq_chunk(BIG.c(c), c, KC)
                postnorm_residual(l, C_MIXPOST, xsrc, xres, lambda c: None)
                store_big(xmid, "xmid", 0, tt)
                norm_from_big(l, C_FFNPRE)
                ngrp = (FFC + 7) // 8
                for fg in range(ngrp):
                    k0 = fg * 8
                    kn = min(8, FFC - k0)
                    ft = FT[fg % 2]
                    for b2 in range(kn // 2):
                        gb_, _ = gemm_fm_block(W["gate", l], k0 * 128 + b2 * 256, ht_main, wkey=("gate", l))
                        ub_, _ = gemm_fm_block(W["up", l], k0 * 128 + b2 * 256, ht_main, wkey=("up", l))
                        for i in range(2):
                            sg = SQ[rot("sq", 2)].all()
                            S.op("act", lambda e, i=i, gb_=gb_, sg=sg: e.activation(out=sg.ap, in_=gb_[i].ap, func=AF.Silu), reads=[gb_[i]], writes=[sg])
                            fc = ft.c(b2 * 2 + i)
                            S.op("dve", lambda e, i=i, ub_=ub_, sg=sg, fc=fc: e.tensor_tensor(out=fc.ap, in0=sg.ap, in1=ub_[i].ap, op=ALU.mult),
                                 reads=[sg, ub_[i]], writes=[fc])
                    wd = W["down", l].rearrange("(c p) n -> p c n", p=128)
                    for nb in range(8):
                        slot = Wst.next(wd[:, k0:k0 + kn, nb * 512:(nb + 1) * 512], kn, 512, wres(("down", l)))
                        for jj in range(4):
                            c = nb * 4 + jj
                            b = gbank()
                            for kl in range(kn):
                                fc = ft.c(kl)
                                S.op("pe", lambda e, jj=jj, kl=kl, fc=fc, slot=slot, b=b: e.matmul(
                                    b.ap, lhsT=slot.ap[:, kl, jj * 128:(jj + 1) * 128], rhs=fc.ap, start=(kl == 0), stop=(kl == kn - 1)),
                                    reads=[slot, fc], writes=[b])
                            if fg == 0:
                                S.op("act", lambda e, c=c, b=b: e.activation(out=BIG.c(c).ap, in_=b.ap, func=AF.Copy), reads=[b], writes=[BIG.c(c)])
                            else:
                                S.op("dve", lambda e, c=c, b=b: e.tensor_tensor(out=BIG.c(c).ap, in0=BIG.c(c).ap, in1=b.ap, op=ALU.add),
                                     reads=[BIG.c(c), b], writes=[BIG.c(c)])
                for c in range(KC):
                    sumsq_chunk(BIG.c(c), c, KC)

                def to_x2b(c):
                    S.op("act", lambda e, c=c: e.activation(out=HT.c(c, 0, T).ap, in_=BIG.c(c).ap, func=AF.Copy), reads=[BIG.c(c)], writes=[HT.c(c, 0, T)])
                postnorm_residual(l, C_FFNPOST, xmid, lambda tt_, g_: dres("xmid", 0, tt_, g_), to_x2b)
                store_big(xmid2, "xmid2", 0, tt)
                pv = W["p", l].rearrange("(c p) t -> p c t", p=128)
                S.op("pool", lambda e, tt=tt: e.dma_start(out=PB.all().ap, in_=pv[:, :, tt * T:(tt + 1) * T]), writes=[PB.all()], dma="pb")
                for blk in range(16):
                    gb_, _ = gemm_fm_block(W["plegate", l], blk * 256, ht_main, wkey=("plegate", l))
                    eb_, _ = gemm_fm_block(W["plein", l], blk * 256, lambda kc: PB.c(kc), kcs=2, wkey=("plein", l))
                    for i in range(2):
                        c = blk * 2 + i
                        sg = SQ[rot("sq", 2)].all()
                        S.op("act", lambda e, i=i, gb_=gb_, sg=sg: e.activation(out=sg.ap, in_=gb_[i].ap, func=AF.Sigmoid), reads=[gb_[i]], writes=[sg])
                        S.op("dve", lambda e, i=i, c=c, eb_=eb_, sg=sg: e.tensor_tensor(out=BIG.c(c).ap, in0=sg.ap, in1=eb_[i].ap, op=ALU.mult),
                             reads=[sg, eb_[i]], writes=[BIG.c(c)])
                        sumsq_chunk(BIG.c(c), c, KC)
                postnorm_residual(l, C_PLE, xmid2, lambda tt_, g_: dres("xmid2", 0, tt_, g_), lambda c: None)
                store_big(xdst, "x", l + 1, tt, final=(role_of(xdst) == "out"))

        def gather_weight(key, l, idx):
            ws, wi, wf, rs = self.wshard[key, l]
            pp = 1
            for cand in range(128, 0, -1):
                if rs % cand == 0:
                    pp = cand
                    break
            srcv = ws.rearrange("(p a) n -> p a n", p=pp)
            dstv = wi.rearrange("(p a) n -> p a n", p=pp)
            S.op("sp", lambda e: e.dma_start(out=dstv, in_=srcv), writes=[dres("wi", key, l)], dma=("wcp", idx % 2))
            grp = [list(range(NCORES))]
            S.op("pool", lambda e: e.collective_compute("AllGather", ALU.bypass, replica_groups=grp, ins=[wi[:, :]], outs=[wf[:, :]]),
                 reads=[dres("wi", key, l)], writes=[dres("wf", (key, l))], dma=("ccw", idx % 2), inc=1)

        if mode == "A":
            phase_A(self.layer)
        elif mode == "B":
            phase_B(self.layer)
        else:
            rest = ("pool", "out", "gate", "up", "down", "plein", "plegate")
            gather_weight("in", 0, 0)
            phase_A(0)
            phase_gather(0)
            for i, k in enumerate(rest):
                gather_weight(k, 0, i + 1)
            gather_weight("in", 1, 8)
            phase_B(0)
            phase_A(1)
            phase_gather(1)
            for i, k in enumerate(rest):
                gather_weight(k, 1, i + 9)
            phase_B(1)


def _fm(a):
    return [np.ascontiguousarray(a[c * TOK:(c + 1) * TOK].T) for c in range(NCORES)]


def _tables():
    seq = SEQ
    t = np.arange(seq)
    row = (t // 64).astype(np.float32)
    col = (t % 64).astype(np.float32)
    inv_freq = (np.float32(10000.0) ** (-np.arange(0, 64, 2, dtype=np.float32) / np.float32(64))).astype(np.float32)
    ang_r = (row[:, None] * inv_freq[None, :]).astype(np.float32)
    ang_c = (col[:, None] * inv_freq[None, :]).astype(np.float32)
    ang = np.concatenate([ang_r, ang_r, ang_c, ang_c], axis=1)
    cosT = np.ascontiguousarray(np.cos(ang).astype(np.float32).T)
    sinT = np.ascontiguousarray(np.sin(ang).astype(np.float32).T)
    R = np.zeros((128, 128), np.float32)
    for base in (0, 64):
        for i in range(32):
            R[base + 32 + i, base + i] = -1.0
            R[base + i, base + 32 + i] = 1.0
    invc = np.zeros((4, seq), np.float32)
    for g, w in enumerate((2, 4, 8, 16)):
        lo = np.clip(t - w // 2, 0, seq)
        hi = np.clip(t + w // 2, 0, seq)
        invc[g] = (np.float32(1.0) / (hi - lo).astype(np.float32))
    return cosT, sinT, R, invc


def _cst(inputs):
    out = np.zeros((128, 2 * CL), np.float32)
    for l in range(2):
        b = l * CL
        for name, c0 in (("norm_mix_pre", C_MIXPRE), ("norm_mix_post", C_MIXPOST), ("norm_ffn_pre", C_FFNPRE),
                         ("norm_ffn_post", C_FFNPOST), ("norm_ple", C_PLE), ("pool_scale", C_PSCALE)):
            out[:, b + c0:b + c0 + 32] = np.asarray(inputs[name][l], np.float32).reshape(32, 128).T
        out[:, b + C_QN] = np.asarray(inputs["q_norm"][l], np.float32)
        out[:, b + C_KN] = np.asarray(inputs["k_norm"][l], np.float32)
    return out


_PROGS = {}


def _get_prog(mode, layer=None):
    key = (mode, layer)
    if key not in _PROGS:
        p = Prog(mode, layer)
        p.build()
        _PROGS[key] = p
    return _PROGS[key]


MODE = "U"


def kernel(**inputs):
    x = np.asarray(inputs["x"], np.float32)[0]
    p = np.asarray(inputs["p"], np.float32)[:, 0]
    cosT, sinT, R, invc = _tables()
    cst = _cst(inputs)
    xT0 = _fm(x)
    pT = [_fm(p[l]) for l in range(2)]
    common = []
    for c in range(NCORES):
        sel = np.zeros((128, 24), np.float32)
        if c > 0:
            sel[:, c - 1] = 1.0
        if c < NCORES - 1:
            sel[:, 8 + c + 1] = 1.0
        sel[:, 16 + c] = 1.0
        common.append({
            "cst": cst, "rotT": R,
            "cosT": np.ascontiguousarray(cosT[:, c * TOK:(c + 1) * TOK]),
            "sinT": np.ascontiguousarray(sinT[:, c * TOK:(c + 1) * TOK]),
            "invc": np.ascontiguousarray(np.broadcast_to(invc[None, :, c * TOK:(c + 1) * TOK], (128, 4, TOK))),
            "sel": sel,
        })

    def wts(l):
        return {
            "w_in%d" % l: np.asarray(inputs["w_in"][l], np.float32),
            "w_pool%d" % l: np.asarray(inputs["w_pool"][l], np.float32).reshape(2048, 1024),
            "w_out%d" % l: np.asarray(inputs["w_out"][l], np.float32),
            "w_gate%d" % l: np.asarray(inputs["w_ffn_gate"][l], np.float32),
            "w_up%d" % l: np.asarray(inputs["w_ffn_up"][l], np.float32),
            "w_down%d" % l: np.asarray(inputs["w_ffn_down"][l], np.float32),
            "w_plein%d" % l: np.asarray(inputs["w_ple_in"][l], np.float32),
            "w_plegate%d" % l: np.asarray(inputs["w_ple_gate"][l], np.float32),
        }

    def launch(prog, maps):
        in_maps = [{n: m[n] for n in prog.in_names} for m in maps]
        res = run_bass_kernel_spmd(prog.nc, in_maps, core_ids=list(range(NCORES)))
        return res.results

    if MODE == "F":
        prog = _get_prog("F")
        maps = []
        for c in range(NCORES):
            m = dict(common[c])
            for l in range(2):
                for nm, arr in wts(l).items():
                    a2 = arr.reshape(-1, arr.shape[-1])
                    rs = a2.shape[0] // NCORES
                    m[nm] = np.ascontiguousarray(a2[c * rs:(c + 1) * rs])
            m["xT0"] = xT0[c]
            m["pT0"] = pT[0][c]
            m["pT1"] = pT[1][c]
            maps.append(m)
        res = launch(prog, maps)
        outT = [res[c]["xT2"] for c in range(NCORES)]
    else:
        xcur = xT0
        for l in range(2):
            pa = _get_prog("A", l)
            wkv = np.ascontiguousarray(np.asarray(inputs["w_in"][l], np.float32)[:, O1:O3])
            maps = []
            for c in range(NCORES):
                m = dict(common[c])
                m["xT%d" % l] = xcur[c]
                m["w_kv%d" % l] = wkv
                maps.append(m)
            ra = launch(pa, maps)
            kvall = np.concatenate([ra[c]["kvsrc%d" % l] for c in range(NCORES)], axis=0)
            xhall = np.concatenate([ra[c]["xhsrc%d" % l] for c in range(NCORES)], axis=0)
            pb = _get_prog("B", l)
            wl = wts(l)
            maps = []
            for c in range(NCORES):
                m = dict(common[c])
                m.update(wl)
                m["xT%d" % l] = xcur[c]
                m["pT%d" % l] = pT[l][c]
                m["kvall%d" % l] = kvall
                m["xhall%d" % l] = xhall
                maps.append(m)
            rb = launch(pb, maps)
            xcur = [rb[c]["xT%d" % (l + 1)] for c in range(NCORES)]
        outT = xcur
    out = np.concatenate([o.T for o in outT], axis=0)[None]
    return np.ascontiguousarray(out.astype(np.float32))
```
